# Optimizing a Trainium2 kernel written in Bass

```python
import math
import jax, jax.numpy as jnp
from jax import lax
import numpy as np

D_MODEL = 1024
BATCH = 8
SEQ = 4096
DEPTH = 1

HEAD_DIM = 64
DILATED_GROUPS = ((128, 1), (512, 4), (2048, 16))
HEADS_PER_GROUP = 4
N_HEADS_A = HEADS_PER_GROUP * len(DILATED_GROUPS)
N_HEADS_B = 8
A_QKV = N_HEADS_A * HEAD_DIM
B_QKV = N_HEADS_B * HEAD_DIM
A_OUT = HEADS_PER_GROUP * HEAD_DIM
N_IN = 3 * A_QKV + 3 * B_QKV + 2 * D_MODEL
BLOCK = 128
NUM_BUCKETS = 32
MAX_DISTANCE = 2048
D_FF = 2816
CONV_WIDTH = 3
EPS = 1e-6

kernel_name = "hybrid_dilated_stickbreaking_block"


def rms_norm(x, g):
    xf = x.astype(jnp.float32)
    y = xf * lax.rsqrt(jnp.mean(xf * xf, axis=-1, keepdims=True) + EPS)
    return (y * g.astype(jnp.float32)).astype(x.dtype)


def t5_bucket(dist):
    max_exact = NUM_BUCKETS // 2
    n = np.asarray(dist, dtype=np.float32)
    large = max_exact + (np.log(np.maximum(n, 1.0) / max_exact)
                         / np.log(MAX_DISTANCE / max_exact)
                         * (NUM_BUCKETS - max_exact)).astype(np.int32)
    large = np.minimum(large, NUM_BUCKETS - 1)
    return np.where(n < max_exact, n.astype(np.int32), large).astype(np.int32)


def dilated_window_attention(q, k, v, bias, window, dilation):
    B, S, H, Dh = q.shape
    n_back = window // dilation
    unit = dilation * BLOCK
    Sp = -(-S // unit) * unit
    nb = Sp // unit

    def to_blocks(t):
        t = jnp.pad(t.astype(jnp.float32), ((0, 0), (0, Sp - S), (0, 0), (0, 0)))
        return t.reshape(B, nb, BLOCK, dilation, H, Dh)

    qb, kb, vb = to_blocks(q), to_blocks(k), to_blocks(v)

    def band(t):
        prev = jnp.concatenate([jnp.zeros_like(t[:, :1]), t[:, :-1]], axis=1)
        return jnp.concatenate([prev, t], axis=2)

    kband, vband = band(kb), band(vb)
    logits = jnp.einsum('bnirhe,bncrhe->bnrhic', qb, kband) * (1.0 / math.sqrt(Dh))

    i = np.arange(BLOCK)[:, None]
    c = np.arange(2 * BLOCK)[None, :]
    j = i + BLOCK - c
    local = (j >= 0) & (j <= n_back)
    key_m = np.arange(nb)[:, None, None] * BLOCK - BLOCK + c[None]
    mask = local[None] & (key_m >= 0)
    bias_band = bias.astype(jnp.float32)[:, np.clip(j, 0, n_back)]

    logits = jnp.where(mask[None, :, None, None], logits + bias_band[None, None, None], -jnp.inf)
    mx = jnp.max(logits, axis=-1, keepdims=True)
    e = jnp.exp(logits - mx)
    den = jnp.sum(e, axis=-1, keepdims=True)
    o = jnp.einsum('bnrhic,bncrhe->bnirhe', e, vband) / den.transpose(0, 1, 4, 2, 3, 5)
    lse = (mx + jnp.log(den))[..., 0].transpose(0, 1, 4, 2, 3)
    o = o.reshape(B, Sp, H, Dh)[:, :S]
    lse = lse.reshape(B, Sp, H)[:, :S]
    return o, lse


def stick_breaking_attention(q, k, v):
    B, S, H, Dh = q.shape
    nb = S // BLOCK
    kf = k.astype(jnp.float32)
    vf = v.astype(jnp.float32)
    qblocks = q.astype(jnp.float32).reshape(B, nb, BLOCK, H, Dh).transpose(1, 0, 2, 3, 4)
    pos_k = jnp.arange(S)

    def one_block(args):
        qblk, n = args
        z = jnp.einsum('bqhe,bkhe->bhqk', qblk, kf) * (1.0 / math.sqrt(Dh))
        t = n * BLOCK + jnp.arange(BLOCK)
        mask = pos_k[None, :] < t[:, None]
        log_beta = jax.nn.log_sigmoid(z)
        log_1m = jnp.where(mask, log_beta - z, 0.0)
        tail = lax.cumsum(log_1m, axis=3, reverse=True) - log_1m
        w = jnp.where(mask, jnp.exp(log_beta + tail), 0.0)
        return jnp.einsum('bhqk,bkhe->bqhe', w, vf)

    out = lax.map(one_block, (qblocks, jnp.arange(nb)))
    return out.transpose(1, 0, 2, 3, 4).reshape(B, S, H, Dh)


def setup_inputs(seed: int = 0) -> dict:
    key = jax.random.key(seed)
    ks = jax.random.split(key, 16)
    nrm = jax.random.normal
    x = nrm(ks[0], (BATCH, SEQ, D_MODEL), jnp.float32)
    g_pre_mix = 1.0 + 0.05 * nrm(ks[1], (DEPTH, D_MODEL), jnp.float32)
    w_in = nrm(ks[2], (DEPTH, D_MODEL, N_IN), jnp.float32) * D_MODEL ** -0.5
    rel_bias = 0.5 * nrm(ks[3], (NUM_BUCKETS, N_HEADS_A), jnp.float32)
    w_branch_a = nrm(ks[4], (DEPTH, A_OUT, D_MODEL), jnp.float32) * A_OUT ** -0.5
    w_branch_b = nrm(ks[5], (DEPTH, B_QKV, D_MODEL), jnp.float32) * B_QKV ** -0.5
    w_out = nrm(ks[6], (DEPTH, D_MODEL, D_MODEL), jnp.float32) * D_MODEL ** -0.5
    g_post_mix = 1.0 + 0.05 * nrm(ks[7], (DEPTH, D_MODEL), jnp.float32)
    g_pre_ffn = 1.0 + 0.05 * nrm(ks[8], (DEPTH, D_MODEL), jnp.float32)
    w_up = nrm(ks[9], (DEPTH, D_MODEL, 2 * D_FF), jnp.float32) * D_MODEL ** -0.5
    conv_w = nrm(ks[10], (DEPTH, CONV_WIDTH, 2 * D_FF), jnp.float32) * CONV_WIDTH ** -0.5
    conv_b = 0.02 * nrm(ks[11], (DEPTH, 2 * D_FF), jnp.float32)
    w_down = nrm(ks[12], (DEPTH, D_FF, D_MODEL), jnp.float32) * D_FF ** -0.5
    g_post_ffn = 1.0 + 0.05 * nrm(ks[13], (DEPTH, D_MODEL), jnp.float32)
    return {"x": x, "g_pre_mix": g_pre_mix, "w_in": w_in, "rel_bias": rel_bias,
            "w_branch_a": w_branch_a, "w_branch_b": w_branch_b, "w_out": w_out,
            "g_post_mix": g_post_mix, "g_pre_ffn": g_pre_ffn, "w_up": w_up,
            "conv_w": conv_w, "conv_b": conv_b, "w_down": w_down, "g_post_ffn": g_post_ffn}


def reference(x, g_pre_mix, w_in, rel_bias, w_branch_a, w_branch_b, w_out, g_post_mix,
              g_pre_ffn, w_up, conv_w, conv_b, w_down, g_post_ffn):
    B, S, D = x.shape
    splits = np.cumsum([A_QKV, A_QKV, A_QKV, B_QKV, B_QKV, B_QKV, D_MODEL]).tolist()
    for l in range(DEPTH):
        h = rms_norm(x, g_pre_mix[l])
        proj = h @ w_in[l]
        qa, ka, va, qb, kb, vb, gate_a, gate_b = jnp.split(proj, splits, axis=-1)
        qa = qa.reshape(B, S, N_HEADS_A, HEAD_DIM)
        ka = ka.reshape(B, S, N_HEADS_A, HEAD_DIM)
        va = va.reshape(B, S, N_HEADS_A, HEAD_DIM)
        qb = qb.reshape(B, S, N_HEADS_B, HEAD_DIM)
        kb = kb.reshape(B, S, N_HEADS_B, HEAD_DIM)
        vb = vb.reshape(B, S, N_HEADS_B, HEAD_DIM)

        outs, lses = [], []
        for g, (window, dilation) in enumerate(DILATED_GROUPS):
            sl = slice(g * HEADS_PER_GROUP, (g + 1) * HEADS_PER_GROUP)
            buckets = t5_bucket(np.arange(window // dilation + 1) * dilation)
            bias = rel_bias[buckets][:, sl].T
            o_g, lse_g = dilated_window_attention(qa[:, :, sl], ka[:, :, sl], va[:, :, sl],
                                                  bias, window, dilation)
            outs.append(o_g)
            lses.append(lse_g)
        wts = jax.nn.softmax(jnp.stack(lses, axis=0), axis=0)
        ya = jnp.sum(wts[..., None] * jnp.stack(outs, axis=0), axis=0).reshape(B, S, A_OUT)

        yb = stick_breaking_attention(qb, kb, vb).reshape(B, S, B_QKV)

        merged = (jax.nn.sigmoid(gate_a.astype(jnp.float32)) * (ya @ w_branch_a[l])
                  + jax.nn.sigmoid(gate_b.astype(jnp.float32)) * (yb @ w_branch_b[l]))
        x = x + rms_norm(merged @ w_out[l], g_post_mix[l]).astype(x.dtype)

        h = rms_norm(x, g_pre_ffn[l])
        u = h @ w_up[l]
        up = jnp.pad(u, ((0, 0), (CONV_WIDTH - 1, 0), (0, 0)))
        cw = conv_w[l]
        u = conv_b[l] + sum(cw[t] * up[:, t:t + S] for t in range(CONV_WIDTH))
        gate, val = jnp.split(u, 2, axis=-1)
        y = (jax.nn.gelu(gate.astype(jnp.float32), approximate=True) * val) @ w_down[l]
        x = x + rms_norm(y, g_post_ffn[l]).astype(x.dtype)
    return x
```

```python
import numpy as np
import concourse.bass as bass
import concourse.mybir as mybir
from concourse.bass_utils import run_bass_kernel_spmd
from contextlib import ExitStack

F32 = mybir.dt.float32
BF16 = mybir.dt.bfloat16
AF = mybir.ActivationFunctionType
ALU = mybir.AluOpType
AX = mybir.AxisListType

SEQ = 4096
D = 1024
NSUB = SEQ // 128
NT = SEQ // 512
N_IN = 5888
D_FF = 2816
NEGV = -30000.0
EPS = 1e-6
DIL = (1, 4, 16)


def sl(start, count, step=1):
    return slice(start, start + (count - 1) * step + 1, step)


class Sched:
    ENGS = ("pe", "act", "dve", "pool", "sp")

    def __init__(self, nc, stack):
        self.nc = nc
        self.stack = stack
        self.ops = []
        self.res = {}
        self.sems = {}
        self.cnt = {}
        self.waited = {e: {} for e in self.ENGS}
        self.emitted = 0
        for e in self.ENGS:
            self.sems[e] = stack.enter_context(nc.semaphore("sem_" + e))
            self.cnt[e] = 0

    def op(self, eng, fn, reads=(), writes=(), stream=None, ndma=1):
        idx = len(self.ops)
        deps = {}
        for r in reads:
            st = self.res.get(r)
            if st and st["w"] is not None:
                deps[st["w"]] = "raw"
        for w in writes:
            st = self.res.get(w)
            if st:
                if st["w"] is not None and st["w"] not in deps:
                    deps[st["w"]] = "waw"
                for rd in st["r"]:
                    if rd not in deps:
                        deps[rd] = "war"
        for d, kind in list(deps.items()):
            p = self.ops[d]
            if p["stream"] is None and p["eng"] == eng and kind != "raw":
                del deps[d]
        o = dict(eng=eng, fn=fn, deps=deps, stream=stream, ndma=ndma, signal=False, val=None, sem=None)
        for d in deps:
            self.ops[d]["signal"] = True
            assert d >= self.emitted or self.ops[d]["val"] is not None, "dep on emitted non-signalling op"
        self.ops.append(o)
        for r in reads:
            st = self.res.setdefault(r, {"w": None, "r": []})
            st["r"].append(idx)
        for w in writes:
            st = self.res.setdefault(w, {"w": None, "r": []})
            st["w"] = idx
            st["r"] = []
        return idx

    def emit(self, last=False):
        nc = self.nc
        ops = self.ops
        lo = self.emitted
        hi = len(ops)
        live = set()
        for st_ in self.res.values():
            if st_["w"] is not None:
                live.add(id(ops[st_["w"]]))
            for r_ in st_["r"]:
                live.add(id(ops[r_]))
        for o in ops[lo:hi]:
            if o["stream"] is not None:
                s = o["stream"]
                if s not in self.sems:
                    self.sems[s] = self.stack.enter_context(nc.semaphore("st_" + str(s)))
                    self.cnt[s] = 0
                self.cnt[s] += 16 * o["ndma"]
                o["val"] = self.cnt[s]
                o["sem"] = s
            else:
                if o["signal"] or (not last and id(o) in live):
                    o["signal"] = True
                    self.cnt[o["eng"]] += 1
                    o["val"] = self.cnt[o["eng"]]
                    o["sem"] = o["eng"]
        sems = self.sems
        with nc.Block() as block:
            def run(engname, eng):
                waited = self.waited[engname]
                for o in ops[lo:hi]:
                    if o["eng"] != engname:
                        continue
                    need = {}
                    for d in o["deps"]:
                        p = ops[d]
                        need[p["sem"]] = max(need.get(p["sem"], 0), p["val"])
                    for s, v in need.items():
                        if waited.get(s, 0) < v:
                            eng.wait_ge(sems[s], v)
                            waited[s] = v
                    ins = o["fn"](eng)
                    if not isinstance(ins, (list, tuple)):
                        ins = [ins]
                    if o["stream"] is not None:
                        assert len(ins) == o["ndma"], (len(ins), o["ndma"])
                        for i_ in ins:
                            i_.then_inc(sems[o["stream"]], 16)
                    elif o["signal"]:
                        ins[-1].then_inc(sems[o["eng"]], 1)
                if engname == "sp" and last:
                    for s, v in self.cnt.items():
                        if v > 0 and waited.get(s, 0) < v:
                            eng.wait_ge(sems[s], v)
                            waited[s] = v

            @block.tensor
            def _(e):
                run("pe", e)

            @block.scalar
            def _(e):
                run("act", e)

            @block.vector
            def _(e):
                run("dve", e)

            @block.gpsimd
            def _(e):
                run("pool", e)

            @block.sync
            def _(e):
                run("sp", e)
        self.emitted = hi


def build(dbg=False, phases=3, sbt=None):
    nc = bass.Bass("TRN2", target_bir_lowering=False)

    def din(name, shape, dt=F32):
        return nc.dram_tensor(name, shape, dt, kind="ExternalInput").ap()

    x = din("x", [SEQ, D])
    w_in = din("w_in", [D, N_IN])
    w_a = din("w_a", [256, D])
    w_b = din("w_b", [512, D])
    w_out = din("w_out", [D, D])
    w_up = din("w_up", [D, 2 * D_FF])
    w_down = din("w_down", [D_FF, D])
    cst = din("cst", [128, 3, 128])
    bm_d = din("bm", [128, 12, 256])
    pv_d = din("pv", [128, 192])
    grow = din("grow", [2, D])
    y = nc.dram_tensor("y", [SEQ, D], F32, kind="ExternalOutput").ap()
    dbg_t = {}
    if dbg:
        dbg_t["hT"] = nc.dram_tensor("d_hT", [128, 8, SEQ], BF16, kind="ExternalOutput").ap()
        dbg_t["ybT"] = nc.dram_tensor("d_ybT", [128, 4, SEQ], BF16, kind="ExternalOutput").ap()
        dbg_t["so"] = nc.dram_tensor("d_so", [3, SEQ, 4, 65], F32, kind="ExternalOutput").ap()
        dbg_t["misc"] = nc.dram_tensor("d_misc", [128, 12, 512], F32, kind="ExternalOutput").ap()
    w_in_b = nc.dram_tensor("w_in_b", [D, N_IN], BF16).ap()
    w_a_b = nc.dram_tensor("w_a_b", [256, D], BF16).ap()
    w_b_b = nc.dram_tensor("w_b_b", [512, D], BF16).ap()
    w_out_b = nc.dram_tensor("w_out_b", [D, D], BF16).ap()
    w_up_b = nc.dram_tensor("w_up_b", [D, 2 * D_FF], BF16).ap()
    w_down_b = nc.dram_tensor("w_down_b", [D_FF, D], BF16).ap()
    so = dbg_t["so"] if dbg else nc.dram_tensor("so", [3, SEQ, 4, 65], F32).ap()

    with ExitStack() as gst:
        S = Sched(nc, gst)

        def sb(stk, name, shape, dt):
            return stk.enter_context(nc.sbuf_tensor(name, shape, dt))

        hT = sb(gst, "hT", [128, 8, SEQ], BF16)
        ybT = sb(gst, "ybT", [128, 4, SEQ], BF16)
        xt = [sb(gst, f"xt{i}", [128, D], F32) for i in range(2)]
        xs = sb(gst, "xs", [128, D], BF16)
        junk = sb(gst, "junk", [128, D], BF16)
        cstb = sb(gst, "cstb", [128, 3, 128], BF16)
        neg1 = sb(gst, "neg1", [128, 128], BF16)
        zer = sb(gst, "zer", [128, 128], BF16)
        ones = sb(gst, "ones", [128, 512], BF16)
        pv = sb(gst, "pv_sb", [128, 192], F32)
        ss = sb(gst, "ss", [128, 64], F32)
        pb = [gst.enter_context(nc.psum_tensor(f"pb{i}", [128, 512], F32)) for i in range(8)]
        pbh = [p[:, :].bitcast(BF16) for p in pb]
        ident = cstb[:, 0, :]
        tri = cstb[:, 1, :]
        maskneg = cstb[:, 2, :]
        gpm = pv[:, 0:8]
        gpf = pv[:, 8:16]
        cw = pv[:, 16:148]
        cb = pv[:, 148:192]

        S.op("pool", lambda e: e.dma_start(out=cstb[:], in_=cst[:, :, :]), writes=["cstb"], stream="ld_cst")
        S.op("sp", lambda e: e.dma_start(out=pv[:], in_=pv_d[:, :]), writes=["pv"], stream="ld_pv")
        S.op("dve", lambda e: e.memset(neg1[:], -1.0), writes=["neg1"])
        S.op("dve", lambda e: e.memset(zer[:], 0.0), writes=["zer"])
        S.op("dve", lambda e: e.memset(ones[:], 1.0), writes=["ones"])
        S.op("dve", lambda e: e.memset(ss[:], 0.0), writes=["ss"])

        def conv_w(name, src, dst, rows, cols, colsplit=1):
            nchunk = rows // 128
            cw_ = cols // colsplit

            def fn(e):
                out = []
                for i in range(nchunk):
                    for c in range(colsplit):
                        out.append(e.dma_start(out=dst[i * 128:(i + 1) * 128, c * cw_:(c + 1) * cw_],
                                               in_=src[i * 128:(i + 1) * 128, c * cw_:(c + 1) * cw_]))
                return out
            S.op("pool", fn, writes=[name], stream="cv_" + name, ndma=nchunk * colsplit)

        conv_w("w_in_b", w_in, w_in_b, D, N_IN, 2)
        conv_w("w_a_b", w_a, w_a_b, 256, D)
        conv_w("w_b_b", w_b, w_b_b, 512, D)
        conv_w("w_out_b", w_out, w_out_b, D, D)
        conv_w("w_up_b", w_up, w_up_b, D, 2 * D_FF, 2)
        conv_w("w_down_b", w_down, w_down_b, D_FF, D)

        def rms_scale(src_res, sq_srcs, col):
            n = len(sq_srcs)
            S.op("pool", lambda e: e.memset(ss[:, col:col + 2], 0.0), writes=[("ss", col), ("ss", col + 1)])
            for k, (ap, width) in enumerate(sq_srcs):
                S.op("act", lambda e, ap=ap, k=k, width=width: e.activation(out=junk[:, 0:width], in_=ap, func=AF.Square,
                                                                             accum_out=ss[:, col + k:col + k + 1]),
                     reads=src_res, writes=["junk", ("ss", col + k)])
            if n == 2:
                S.op("dve", lambda e: e.tensor_tensor(out=ss[:, col + 2:col + 3], in0=ss[:, col:col + 1], in1=ss[:, col + 1:col + 2], op=ALU.add),
                     reads=[("ss", col), ("ss", col + 1)], writes=[("ss", col + 2)])
                srcc = col + 2
            else:
                srcc = col
            S.op("act", lambda e: e.activation(out=ss[:, col + 3:col + 4], in_=ss[:, srcc:srcc + 1], func=AF.Ln, bias=epsb[:, 0:1], scale=1.0 / D),
                 reads=[("ss", srcc), "epsb"], writes=[("ss", col + 3)])
            S.op("act", lambda e: e.activation(out=ss[:, col + 3:col + 4], in_=ss[:, col + 3:col + 4], func=AF.Exp, scale=-0.5),
                 reads=[("ss", col + 3)], writes=[("ss", col + 3)])
            return ss[:, col + 3:col + 4], ("ss", col + 3)

        epsb = sb(gst, "epsb", [128, 1], F32)
        S.op("dve", lambda e: e.memset(epsb[:], EPS), writes=["epsb"])

        def transposes8(src, src_res, bank, ncol=8):
            pT = pbh[bank]

            def fn(e):
                return [e.transpose(out=pT[:, k * 128:(k + 1) * 128], in_=src[:, k * 128:(k + 1) * 128], identity=ident) for k in range(ncol)]
            S.op("pe", fn, reads=[src_res, "cstb"], writes=[f"pb{bank}"])

        for i in range(NSUB):
            s_ = i % 2
            S.op("sp", lambda e, i=i, s_=s_: e.dma_start(out=xt[s_][:], in_=x[i * 128:(i + 1) * 128, :]), writes=[f"xt{s_}"], stream=f"ldx{s_}")
            col = (i % 4) * 4
            rs, rs_res = rms_scale([f"xt{s_}"], [(xt[s_][:], D)], col)
            S.op("act", lambda e, s_=s_, rs=rs: e.activation(out=xs[:], in_=xt[s_][:], func=AF.Copy, scale=rs), reads=[f"xt{s_}", rs_res], writes=["xs"])
            bank = i % 2
            transposes8(xs, "xs", bank)
            S.op("dve", lambda e, i=i, bank=bank: e.tensor_tensor(
                out=hT[:, :, i * 128:(i + 1) * 128], in0=pbh[bank].rearrange("p (k t) -> p k t", k=8),
                in1=gpm.rearrange("p (k o) -> p k o", o=1).broadcast_to([128, 8, 128]), op=ALU.mult),
                reads=[f"pb{bank}", "pv"], writes=[("hT", i // 4)])
        S.emit()
        if dbg:
            nc.all_engine_barrier()
            S.op("sp", lambda e: e.dma_start(out=dbg_t["hT"][:, :, :], in_=hT[:]), reads=[("hT", g) for g in range(NT)], stream="dbg_hT")

        if phases >= 2:
            nc.all_engine_barrier()
            with ExitStack() as p2:
                wq = [sb(p2, f"wq{i}", [128, 8, 3, 128], BF16) for i in range(2)]
                QT2 = sb(p2, "QT2", [128, SEQ], BF16)
                KT2 = sb(p2, "KT2", [128, SEQ], BF16)
                V2 = sb(p2, "V2", [128, NSUB, 128], BF16)
                ub = [sb(p2, f"u{i}", [128, 512], F32) for i in range(2)]
                spb = [sb(p2, f"spb{i}", [128, 512], BF16) for i in range(2)]
                wb = [sb(p2, f"wb{i}", [128, 512], BF16) for i in range(3)]
                ls32 = sb(p2, "ls32", [128, 512], F32)
                lsb = [sb(p2, f"lsb{i}", [128, 512], BF16) for i in range(3)]
                bm = sb(p2, "bm_sb", [128, 12, 256], F32)
                sbias = [sb(p2, f"sbias{i}", [128, 256], F32) for i in range(2)]
                eb = [sb(p2, f"eb{i}", [128, 256], BF16) for i in range(2)]
                eT = [sb(p2, f"eT{i}", [128, 256], BF16) for i in range(2)]
                o_all = sb(p2, "o_all", [128, NSUB, 65], F32)
                den_all = sb(p2, "den_all", [128, NSUB], F32)
                mx_all = sb(p2, "mx_all", [128, NSUB], F32)
                rden = sb(p2, "rden", [128, NSUB], F32)
                lnd = sb(p2, "lnd", [128, NSUB], F32)

                S.op("sp", lambda e: e.dma_start(out=bm[:], in_=bm_d[:, :, :]), writes=["bm"], stream="ld_bm")
                allhT = [("hT", g) for g in range(NT)]
                wq_n = [0]

                def qkv_proj(col_ap, strided_d=None):
                    ws = wq_n[0] % 2
                    wq_n[0] += 1
                    W = wq[ws]
                    S.op("sp", lambda e: [e.dma_start(out=W[:, :, t_, :], in_=col_ap[:, :, t_, :]) for t_ in range(3)], reads=["w_in_b"], writes=[f"wq{ws}"], stream=f"ld_wq{ws}", ndma=3)
                    bk = [5, 6, 7]
                    n = [0]

                    def nb():
                        b = bk[n[0] % 3]
                        n[0] += 1
                        return b
                    for G in range(NT):
                        for t, dst, nm in ((0, QT2, "QT2"), (1, KT2, "KT2")):
                            b = nb()
                            S.op("pe", lambda e, G=G, t=t, b=b: [
                                e.matmul(pb[b][:, :], lhsT=W[:, kc, t, :], rhs=hT[:, kc, G * 512:(G + 1) * 512], start=(kc == 0), stop=(kc == 7))
                                for kc in range(8)], reads=[f"wq{ws}", ("hT", G)], writes=[f"pb{b}"])
                            if t == 0:
                                S.op("act", lambda e, G=G, b=b: e.activation(out=QT2[:, G * 512:(G + 1) * 512], in_=pb[b][:, :], func=AF.Copy, scale=0.125),
                                     reads=[f"pb{b}"], writes=["QT2"])
                            else:
                                S.op("dve", lambda e, G=G, b=b: e.tensor_copy(out=KT2[:, G * 512:(G + 1) * 512], in_=pb[b][:, :]),
                                     reads=[f"pb{b}"], writes=["KT2"])
                    d = strided_d or 1
                    nbk = NSUB // d
                    for b4 in range(NSUB // 4):
                        b = nb()

                        def fn(e, b4=b4, b=b):
                            out = []
                            for bb in range(4):
                                blk = b4 * 4 + bb
                                r, n_ = blk // nbk, blk % nbk
                                start = n_ * 128 * d + r
                                for kc in range(8):
                                    out.append(e.matmul(pb[b][:, bb * 128:(bb + 1) * 128], lhsT=hT[:, kc, sl(start, 128, d)], rhs=W[:, kc, 2, :],
                                                        start=(kc == 0), stop=(kc == 7)))
                            return out
                        S.op("pe", fn, reads=[f"wq{ws}"] + allhT, writes=[f"pb{b}"])
                        S.op("dve" if b4 % 2 else "act",
                             (lambda e, b4=b4, b=b: e.tensor_copy(out=V2[:, b4 * 4:(b4 + 1) * 4, :], in_=pb[b][:, :].rearrange("p (a c) -> p a c", a=4))) if b4 % 2 else
                             (lambda e, b4=b4, b=b: e.activation(out=V2[:, b4 * 4:(b4 + 1) * 4, :], in_=pb[b][:, :].rearrange("p (a c) -> p a c", a=4), func=AF.Copy)),
                             reads=[f"pb{b}"], writes=["V2"])

                def sb_pair(hp):
                    col_ap = w_in_b[:, 2304:3840].rearrange("(kc p) (t h c) -> p kc t h c", p=128, t=3, h=4)[:, :, :, hp, :]
                    qkv_proj(col_ap)
                    tiles = []
                    for hh in range(2):
                        for G in range(NT):
                            top = 4 * G + 3
                            for kb in range(top, -1, -1):
                                dd = kb - 4 * G
                                tiles.append(dict(hh=hh, G=G, kb=kb, c0=max(dd, 0) * 128, diag=dd >= 0, first=(kb == top), last=(kb == 0)))
                    if sbt is not None:
                        tiles = tiles[:sbt]
                        tiles[-1]['last'] = True
                    nt_ = len(tiles)

                    def stage1(t):
                        T_ = tiles[t]
                        po, G, kb, c0 = 64 * T_["hh"], T_["G"], T_["kb"], T_["c0"]
                        A, Tb = t % 2, 2 + t % 2
                        for bank, lastflag in ((A, True), (Tb, False)):
                            def fn(e, bank=bank, lastflag=lastflag):
                                out = [e.matmul(pb[bank][:, c0:512], lhsT=KT2[po:po + 64, kb * 128:(kb + 1) * 128],
                                                rhs=QT2[po:po + 64, G * 512 + c0:(G + 1) * 512], start=True, stop=(lastflag and not T_["diag"]))]
                                if T_["diag"]:
                                    out.append(e.matmul(pb[bank][:, c0:c0 + 128], lhsT=ident, rhs=maskneg, start=False, stop=lastflag))
                                return out
                            S.op("pe", fn, reads=["QT2", "KT2", "cstb"], writes=[f"pb{bank}"])
                        S.op("act", lambda e: e.activation(out=ub[t % 2][:, c0:512], in_=pb[A][:, c0:512], func=AF.Exp), reads=[f"pb{A}"], writes=[f"u{t % 2}"])
                        S.op("act", lambda e: e.activation(out=spb[t % 2][:, c0:512], in_=ub[t % 2][:, c0:512], func=AF.Ln, bias=1.0), reads=[f"u{t % 2}"], writes=[f"spb{t % 2}"])
                        if T_["first"]:
                            S.op("pool", lambda e: e.memset(ls32[:], 0.0), writes=["ls32"])
                        if not T_["last"]:
                            c0n = tiles[t + 1]["c0"]
                            S.op("pool", lambda e: e.tensor_tensor(out=ls32[:, c0:512], in0=ls32[:, c0:512], in1=spb[t % 2][:, c0:512], op=ALU.add),
                                 reads=["ls32", f"spb{t % 2}"], writes=["ls32"])
                            S.op("dve", lambda e: e.tensor_copy(out=lsb[(t + 1) % 3][:, c0n:512], in_=ls32[:, c0n:512]), reads=["ls32"], writes=[f"lsb{(t + 1) % 3}"])

                    def stage2(t):
                        T_ = tiles[t]
                        c0 = T_["c0"]
                        Tb = 2 + t % 2

                        def fn(e):
                            out = [e.matmul(pb[Tb][:, c0:512], lhsT=tri, rhs=spb[t % 2][:, c0:512], start=False, stop=T_["first"])]
                            if not T_["first"]:
                                out.append(e.matmul(pb[Tb][:, c0:512], lhsT=neg1[:], rhs=lsb[t % 3][:, c0:512], start=False, stop=True))
                            return out
                        S.op("pe", fn, reads=[f"spb{t % 2}", f"lsb{t % 3}", "cstb", "neg1", f"pb{Tb}"], writes=[f"pb{Tb}"])
                        S.op("act", lambda e: e.activation(out=wb[t % 3][:, c0:512], in_=pb[Tb][:, c0:512], func=AF.Exp), reads=[f"pb{Tb}"], writes=[f"wb{t % 3}"])

                    def stage3(t):
                        T_ = tiles[t]
                        po, G, kb, c0 = 64 * T_["hh"], T_["G"], T_["kb"], T_["c0"]

                        def fn(e):
                            out = []
                            if T_["first"]:
                                out.append(e.matmul(pb[4][:, :], lhsT=zer[:], rhs=ones[:], start=True, stop=False))
                            out.append(e.matmul(pb[4][:, c0:512], lhsT=V2[:, kb, :], rhs=wb[t % 3][:, c0:512], start=False, stop=T_["last"]))
                            return out
                        S.op("pe", fn, reads=["V2", f"wb{t % 3}", "zer", "ones", "pb4"], writes=["pb4"])
                        if T_["last"]:
                            S.op("dve", lambda e: e.tensor_copy(out=ybT[po:po + 64, hp, G * 512:(G + 1) * 512], in_=pb[4][po:po + 64, :]), reads=["pb4"], writes=[("ybT", G)])

                    for t in range(nt_ + 2):
                        if t < nt_:
                            stage1(t)
                        if 0 <= t - 1 < nt_:
                            stage2(t - 1)
                        if 0 <= t - 2 < nt_:
                            stage3(t - 2)

                for hp_ in range(4 if sbt is None else 1):
                    sb_pair(hp_)

                def dil_pair(p_):
                    g = p_ // 2
                    d = DIL[g]
                    nbk = NSUB // d
                    col_ap = w_in_b[:, 0:2304].rearrange("(kc p) (t h c) -> p kc t h c", p=128, t=3, h=6)[:, :, :, p_, :]
                    qkv_proj(col_ap, strided_d=d)
                    def dil_head(hh):
                        h = 2 * p_ + hh
                        s_ = h % 4
                        po = 64 * hh
                        S.op("dve", lambda e: e.memset(den_all[:], 0.0), writes=["den_all"] + [("den", b) for b in range(NSUB)])

                        def stA(blk):
                            r, n_ = blk // nbk, blk % nbk
                            nch = 1 if n_ == 0 else 2
                            nc_ = nch * 128
                            qs = n_ * 128 * d + r
                            ks = (n_ - nch + 1) * 128 * d + r
                            k2 = blk % 2
                            S.op("pe", lambda e: e.matmul(pb[k2][:, 0:nc_], lhsT=QT2[po:po + 64, sl(qs, 128, d)], rhs=KT2[po:po + 64, sl(ks, nc_, d)],
                                                          start=True, stop=True), reads=["QT2", "KT2"], writes=[f"pb{k2}"])
                            S.op("dve", lambda e: e.tensor_tensor(out=sbias[k2][:, 0:nc_], in0=pb[k2][:, 0:nc_], in1=bm[:, h, 256 - nc_:256], op=ALU.add),
                                 reads=[f"pb{k2}", "bm"], writes=[f"sbias{k2}"])
                            S.op("dve", lambda e: e.tensor_reduce(out=mx_all[:, blk:blk + 1], in_=sbias[k2][:, 0:nc_], axis=AX.X, op=ALU.max, negate=True),
                                 reads=[f"sbias{k2}"], writes=[("mx", blk)])
                            S.op("act", lambda e: e.activation(out=eb[k2][:, 0:nc_], in_=sbias[k2][:, 0:nc_], func=AF.Exp, bias=mx_all[:, blk:blk + 1],
                                                               accum_out=den_all[:, blk:blk + 1]),
                                 reads=[f"sbias{k2}", ("mx", blk), "den_all"], writes=[f"eb{k2}", ("den", blk)])

                        def stB(blk):
                            n_ = blk % nbk
                            nch = 1 if n_ == 0 else 2
                            k2 = blk % 2
                            S.op("pe", lambda e: [e.transpose(out=pbh[2 + k2][:, c * 128:(c + 1) * 128], in_=eb[k2][:, c * 128:(c + 1) * 128], identity=ident)
                                                  for c in range(nch)], reads=[f"eb{k2}", "cstb"], writes=[f"pb{2 + k2}"])
                            S.op("dve", lambda e: e.tensor_copy(out=eT[k2][:, 0:nch * 128], in_=pbh[2 + k2][:, 0:nch * 128]), reads=[f"pb{2 + k2}"], writes=[f"eT{k2}"])

                        def stC(blk):
                            n_ = blk % nbk
                            nch = 1 if n_ == 0 else 2
                            k2 = blk % 2
                            S.op("pe", lambda e: [e.matmul(pb[4 + k2][:, 0:64], lhsT=eT[k2][:, c * 128:(c + 1) * 128], rhs=V2[:, blk - nch + 1 + c, po:po + 64],
                                                           start=(c == 0), stop=(c == nch - 1)) for c in range(nch)],
                                 reads=[f"eT{k2}", "V2"], writes=[f"pb{4 + k2}"])
                            S.op("act", lambda e: e.activation(out=o_all[:, blk, 0:64], in_=pb[4 + k2][:, 0:64], func=AF.Copy), reads=[f"pb{4 + k2}", "o_fin"], writes=[("o", blk)])

                        for it in range(NSUB + 2):
                            if it < NSUB:
                                stA(it)
                            if 0 <= it - 1 < NSUB:
                                stB(it - 1)
                            if 0 <= it - 2 < NSUB:
                                stC(it - 2)
                        allo = [("o", b) for b in range(NSUB)]
                        S.op("dve", lambda e: e.reciprocal(out=rden[:], in_=den_all[:]), reads=[("den", b) for b in range(NSUB)], writes=["rden"])
                        S.op("dve", lambda e: e.tensor_tensor(out=o_all[:, :, 0:64], in0=o_all[:, :, 0:64],
                                                              in1=rden[:, :].rearrange("p (b o) -> p b o", o=1).broadcast_to([128, NSUB, 64]), op=ALU.mult),
                             reads=allo + ["rden"], writes=["o_n"])
                        S.op("act", lambda e: e.activation(out=lnd[:], in_=den_all[:], func=AF.Ln), reads=[("den", b) for b in range(NSUB)], writes=["lnd"])
                        S.op("dve", lambda e: e.tensor_tensor(out=o_all[:, :, 64], in0=lnd[:], in1=mx_all[:], op=ALU.subtract),
                             reads=["lnd"] + [("mx", b) for b in range(NSUB)], writes=["o_l"])

                        def fn(e, g=g, s_=s_, d=d, nbk=nbk):
                            out = []
                            dst = so[g, :, s_, :].rearrange("(n i r) e -> i r n e", n=nbk, i=128, r=d)
                            src = o_all[:, :, :].rearrange("p (r n) e -> p r n e", r=d)
                            for r in range(d):
                                out.append(e.dma_start(out=dst[:, r, :, :], in_=src[:, r, :, :]))
                            return out
                        S.op("pool", fn, reads=["o_n", "o_l"], writes=["so", "o_fin"], stream="st_so", ndma=d)
                    for hh_ in range(2):
                        dil_head(hh_)

                for pp_ in range(6 if sbt is None else 0):
                    dil_pair(pp_)
                S.emit()
                if dbg and sbt is not None:
                    nc.all_engine_barrier()
                    dm = dbg_t["misc"]
                    S.op("dve", lambda e: e.tensor_copy(out=ub[0][:], in_=pb[1][:, :]), reads=["pb1"], writes=["u0"])
                    S.op("dve", lambda e: e.tensor_copy(out=ub[1][:], in_=pb[3][:, :]), reads=["pb3"], writes=["u1"])
                    S.op("dve", lambda e: e.tensor_copy(out=ls32[:], in_=pb[4][:, :]), reads=["pb4"], writes=["ls32"])
                    S.emit()
                    nc.all_engine_barrier()
                    srcs = [ub[0][:], ub[1][:], spb[0][:], spb[1][:], lsb[0][:], lsb[1][:], wb[0][:], wb[1][:], wb[2][:], ls32[:], QT2[:, 0:512], KT2[:, 0:512]]
                    S.op("pool", lambda e: [e.dma_start(out=dm[:, i_, :], in_=a_) for i_, a_ in enumerate(srcs)], stream="dbg_misc", ndma=len(srcs))
                    S.emit()
            if dbg:
                nc.all_engine_barrier()
                S.op("sp", lambda e: e.dma_start(out=dbg_t["ybT"][:, :, :], in_=ybT[:]), reads=[("ybT", g) for g in range(NT)], stream="dbg_ybT")

        if phases >= 3:
            nc.all_engine_barrier()
            with ExitStack() as p3:
                ws = [sb(p3, f"ws{i}", [128, 4096], BF16) for i in range(3)]
                mg = sb(p3, "mg", [128, 3, 4, 65], F32)
                sm = sb(p3, "sm", [128, 64], F32)
                yab = sb(p3, "yab", [128, 256], BF16)
                yaT = sb(p3, "yaT", [128, 2, 512], BF16)
                ubf = [sb(p3, f"ub{i}", [128, 514], F32) for i in range(4)]
                mT = sb(p3, "mT", [128, 8, 512], BF16)
                cgv = sb(p3, "cgv", [128, 1024], F32)
                gl = sb(p3, "gl", [128, 512], F32)
                actT = sb(p3, "actT", [128, 22, 512], BF16)
                halo = sb(p3, "halo", [128, 44, 2], F32)
                gpt = sb(p3, "gpt", [128, D], F32)
                wsn = [0]

                def wload(fn_dma, reads, ndma=1):
                    i = wsn[0] % 3
                    wsn[0] += 1
                    S.op("sp", lambda e: fn_dma(e, ws[i]), reads=reads, writes=[f"ws{i}"], stream=f"ld_ws{i}", ndma=ndma)
                    return ws[i], f"ws{i}"

                S.op("pool", lambda e: e.memset(halo[:], 0.0), writes=[("halo", c_) for c_ in range(44)])
                xn = [0]
                def p3_tile(tt):
                    tc0 = tt * 512
                    for st_ in range(4):
                        tok0 = tc0 + st_ * 128
                        S.op("sp", lambda e, tok0=tok0: [e.dma_start(out=mg[:, g_, :, :], in_=so[g_, tok0:tok0 + 128, :, :]) for g_ in range(3)],
                             reads=["so"], writes=["mg"], stream="ld_mg", ndma=3)
                        lse = mg[:, :, :, 64]
                        S.op("dve", lambda e: e.tensor_tensor(out=sm[:, 0:4], in0=mg[:, 0, :, 64], in1=mg[:, 1, :, 64], op=ALU.max), reads=["mg"], writes=["sm_a"])
                        S.op("dve", lambda e: e.tensor_tensor(out=sm[:, 0:4], in0=sm[:, 0:4], in1=mg[:, 2, :, 64], op=ALU.max), reads=["mg", "sm_a"], writes=["sm_b"])
                        S.op("dve", lambda e, lse=lse: e.tensor_tensor(out=sm[:, 4:16].rearrange("p (g s) -> p g s", g=3), in0=lse,
                                                                       in1=sm[:, 0:4].rearrange("p (o s) -> p o s", o=1).broadcast_to([128, 3, 4]), op=ALU.subtract),
                             reads=["mg", "sm_b"], writes=["sm_c"])
                        S.op("act", lambda e: e.activation(out=sm[:, 16:28], in_=sm[:, 4:16], func=AF.Exp), reads=["sm_c"], writes=["sm_d"])
                        S.op("dve", lambda e: e.tensor_tensor(out=sm[:, 28:32], in0=sm[:, 16:20], in1=sm[:, 20:24], op=ALU.add), reads=["sm_d"], writes=["sm_e"])
                        S.op("dve", lambda e: e.tensor_tensor(out=sm[:, 28:32], in0=sm[:, 28:32], in1=sm[:, 24:28], op=ALU.add), reads=["sm_d", "sm_e"], writes=["sm_f"])
                        S.op("dve", lambda e: e.reciprocal(out=sm[:, 32:36], in_=sm[:, 28:32]), reads=["sm_f"], writes=["sm_g"])
                        S.op("dve", lambda e: e.tensor_tensor(out=sm[:, 36:48].rearrange("p (g s) -> p g s", g=3), in0=sm[:, 16:28].rearrange("p (g s) -> p g s", g=3),
                                                              in1=sm[:, 32:36].rearrange("p (o s) -> p o s", o=1).broadcast_to([128, 3, 4]), op=ALU.mult),
                             reads=["sm_d", "sm_g"], writes=["sm_h"])
                        S.op("dve", lambda e: e.tensor_tensor(out=mg[:, :, :, 0:64], in0=mg[:, :, :, 0:64],
                                                              in1=sm[:, 36:48].rearrange("p (g s o) -> p g s o", g=3, o=1).broadcast_to([128, 3, 4, 64]), op=ALU.mult),
                             reads=["mg", "sm_h"], writes=["mg"])
                        S.op("dve", lambda e: e.tensor_tensor(out=mg[:, 0, :, 0:64], in0=mg[:, 0, :, 0:64], in1=mg[:, 1, :, 0:64], op=ALU.add), reads=["mg"], writes=["mg"])
                        S.op("dve", lambda e: e.tensor_tensor(out=yab[:, :].rearrange("p (s c) -> p s c", s=4), in0=mg[:, 0, :, 0:64], in1=mg[:, 2, :, 0:64], op=ALU.add),
                             reads=["mg"], writes=["yab"])
                        transposes8(yab, "yab", 7, ncol=2)
                        S.op("dve", lambda e, st_=st_: e.tensor_copy(out=yaT[:, :, st_ * 128:(st_ + 1) * 128], in_=pbh[7][:, 0:256].rearrange("p (k t) -> p k t", k=2)),
                             reads=["pb7"], writes=["yaT"])
                    for hf in range(2):
                        Wga, rga = wload(lambda e, W, hf=hf: e.dma_start(out=W[:, :].rearrange("p (k n) -> p k n", k=8),
                                                                        in_=w_in_b[:, 3840 + hf * 512:3840 + (hf + 1) * 512].rearrange("(k p) n -> p k n", p=128)), ["w_in_b"])
                        Wgb, rgb = wload(lambda e, W, hf=hf: e.dma_start(out=W[:, :].rearrange("p (k n) -> p k n", k=8),
                                                                        in_=w_in_b[:, 4864 + hf * 512:4864 + (hf + 1) * 512].rearrange("(k p) n -> p k n", p=128)), ["w_in_b"])
                        Wab, rab = wload(lambda e, W, hf=hf: [
                            e.dma_start(out=W[:, 0:1024].rearrange("p (k n) -> p k n", k=2), in_=w_a_b[:, hf * 512:(hf + 1) * 512].rearrange("(k p) n -> p k n", p=128)),
                            e.dma_start(out=W[:, 1024:3072].rearrange("p (k n) -> p k n", k=4), in_=w_b_b[:, hf * 512:(hf + 1) * 512].rearrange("(k p) n -> p k n", p=128))],
                            ["w_a_b", "w_b_b"], ndma=2)
                        Wga3 = Wga[:, :].rearrange("p (k n) -> p k n", k=8)
                        Wgb3 = Wgb[:, :].rearrange("p (k n) -> p k n", k=8)
                        Wab3 = Wab[:, 0:3072].rearrange("p (k n) -> p k n", k=6)
                        for o4 in range(4):
                            oc = hf * 4 + o4
                            b0 = 4 * (oc % 2)
                            cs = slice(o4 * 128, (o4 + 1) * 128)
                            S.op("pe", lambda e, b0=b0, cs=cs, Wga3=Wga3: [e.matmul(pb[b0][:, :], lhsT=Wga3[:, kc, cs], rhs=hT[:, kc, tc0:tc0 + 512], start=(kc == 0), stop=(kc == 7))
                                                                        for kc in range(8)], reads=[rga, ("hT", tt)], writes=[f"pb{b0}"])
                            S.op("pe", lambda e, b0=b0, cs=cs, Wab3=Wab3: [e.matmul(pb[b0 + 1][:, :], lhsT=Wab3[:, kc, cs], rhs=yaT[:, kc, :], start=(kc == 0), stop=(kc == 1))
                                                                        for kc in range(2)], reads=[rab, "yaT"], writes=[f"pb{b0 + 1}"])
                            S.op("pe", lambda e, b0=b0, cs=cs, Wgb3=Wgb3: [e.matmul(pb[b0 + 2][:, :], lhsT=Wgb3[:, kc, cs], rhs=hT[:, kc, tc0:tc0 + 512], start=(kc == 0), stop=(kc == 7))
                                                                        for kc in range(8)], reads=[rgb, ("hT", tt)], writes=[f"pb{b0 + 2}"])
                            S.op("pe", lambda e, b0=b0, cs=cs, Wab3=Wab3: [e.matmul(pb[b0 + 3][:, :], lhsT=Wab3[:, 2 + kc, cs], rhs=ybT[:, kc, tc0:tc0 + 512], start=(kc == 0), stop=(kc == 3))
                                                                        for kc in range(4)], reads=[rab, ("ybT", tt)], writes=[f"pb{b0 + 3}"])
                            S.op("act", lambda e, b0=b0: e.activation(out=ubf[0][:, 0:512], in_=pb[b0][:, :], func=AF.Sigmoid), reads=[f"pb{b0}"], writes=["ub0"])
                            S.op("act", lambda e, b0=b0: e.activation(out=ubf[1][:, 0:512], in_=pb[b0 + 2][:, :], func=AF.Sigmoid), reads=[f"pb{b0 + 2}"], writes=["ub1"])
                            S.op("dve", lambda e, b0=b0: e.tensor_tensor(out=ubf[2][:, 0:512], in0=pb[b0 + 1][:, :], in1=ubf[0][:, 0:512], op=ALU.mult),
                                 reads=[f"pb{b0 + 1}", "ub0"], writes=["ub2"])
                            S.op("dve", lambda e, b0=b0: e.tensor_tensor(out=ubf[3][:, 0:512], in0=pb[b0 + 3][:, :], in1=ubf[1][:, 0:512], op=ALU.mult),
                                 reads=[f"pb{b0 + 3}", "ub1"], writes=["ub3"])
                            S.op("pool", lambda e, oc=oc: e.tensor_tensor(out=mT[:, oc, :], in0=ubf[2][:, 0:512], in1=ubf[3][:, 0:512], op=ALU.add),
                                 reads=["ub2", "ub3"], writes=["mT"])
                    Wo = []
                    for hf in range(2):
                        Wo.append(wload(lambda e, W, hf=hf: e.dma_start(out=W[:, :].rearrange("p (k n) -> p k n", k=8),
                                                                        in_=w_out_b[:, hf * 512:(hf + 1) * 512].rearrange("(k p) n -> p k n", p=128)), ["w_out_b"]))
                    S.op("sp", lambda e: e.dma_start(out=gpt[:], in_=grow[0].partition_broadcast(128)), writes=["gpt"], stream="ld_gpt")
                    for st_ in range(4):
                        sub = tt * 4 + st_
                        r0 = sub * 128
                        b0 = 2 * (st_ % 2)
                        for hf in range(2):
                            W3 = Wo[hf][0][:, :].rearrange("p (k n) -> p k n", k=8)
                            S.op("pe", lambda e, b0=b0, hf=hf, W3=W3, st_=st_: [e.matmul(pb[b0 + hf][:, :], lhsT=mT[:, kc, st_ * 128:(st_ + 1) * 128], rhs=W3[:, kc, :],
                                                                                      start=(kc == 0), stop=(kc == 7)) for kc in range(8)],
                                 reads=["mT", Wo[hf][1]], writes=[f"pb{b0 + hf}"])
                        xi = xn[0] % 2
                        xn[0] += 1
                        S.op("sp", lambda e, xi=xi, r0=r0: e.dma_start(out=xt[xi][:], in_=x[r0:r0 + 128, :]), writes=[f"xt{xi}"], stream=f"ldx{xi}")
                        col = 16 + (st_ % 2) * 8
                        rs, rs_res = rms_scale([f"pb{b0}", f"pb{b0 + 1}"], [(pb[b0][:, :], 512), (pb[b0 + 1][:, :], 512)], col)
                        for hf in range(2):
                            S.op("dve", lambda e, b0=b0, hf=hf, rs=rs: e.scalar_tensor_tensor(out=cgv[:, hf * 512:(hf + 1) * 512], in0=pb[b0 + hf][:, :], scalar=rs,
                                                                                          in1=gpt[:, hf * 512:(hf + 1) * 512], op0=ALU.mult, op1=ALU.mult),
                                 reads=[f"pb{b0 + hf}", rs_res, "gpt"], writes=[("cgv", hf)])
                        S.op("pool", lambda e, xi=xi: e.tensor_tensor(out=xt[xi][:], in0=cgv[:], in1=xt[xi][:], op=ALU.add), reads=[("cgv", 0), ("cgv", 1), f"xt{xi}"], writes=[f"xt{xi}"])
                        S.op("pool", lambda e, xi=xi, r0=r0: e.dma_start(out=y[r0:r0 + 128, :], in_=xt[xi][:]), reads=[f"xt{xi}"], writes=[("y", sub)], stream=f"st_x1_{xi}")
                        rs2, rs2_res = rms_scale([f"xt{xi}"], [(xt[xi][:], D)], col + 4)
                        S.op("act", lambda e, xi=xi, rs2=rs2: e.activation(out=xs[:], in_=xt[xi][:], func=AF.Copy, scale=rs2), reads=[f"xt{xi}", rs2_res], writes=["xs"])
                        tb = 4 + st_ % 2
                        transposes8(xs, "xs", tb)
                        S.op("dve", lambda e, tb=tb, r0=r0: e.tensor_tensor(
                            out=hT[:, :, r0:r0 + 128], in0=pbh[tb].rearrange("p (k t) -> p k t", k=8),
                            in1=gpf.rearrange("p (k o) -> p k o", o=1).broadcast_to([128, 8, 128]), op=ALU.mult),
                            reads=[f"pb{tb}", "pv"], writes=[("hT", tt)])
                    for jp in range(11):
                        j0 = 2 * jp
                        Wu, ru = wload(lambda e, W, j0=j0: [
                            e.dma_start(out=W[:, :].rearrange("p (k v n) -> p k v n", k=8, v=2)[:, :, gv, :],
                                        in_=w_up_b[:, gv * D_FF + j0 * 128:gv * D_FF + (j0 + 2) * 128].rearrange("(k p) n -> p k n", p=128)) for gv in range(2)],
                            ["w_up_b"], ndma=2)
                        Wu4 = Wu[:, :].rearrange("p (k v n) -> p k v n", k=8, v=2)
                        for jj in range(2):
                            j = j0 + jj
                            ua, uv = (0, 1) if jj == 0 else (2, 3)
                            ba, bv = (0, 1) if jj == 0 else (2, 3)
                            for gv, bank in ((0, ba), (1, bv)):
                                S.op("pe", lambda e, gv=gv, bank=bank, jj=jj, Wu4=Wu4: [
                                    e.matmul(pb[bank][:, :], lhsT=Wu4[:, kc, gv, jj * 128:(jj + 1) * 128], rhs=hT[:, kc, tc0:tc0 + 512], start=(kc == 0), stop=(kc == 7))
                                    for kc in range(8)], reads=[ru, ("hT", tt)], writes=[f"pb{bank}"])
                            for gv, u_, bank in ((0, ua, ba), (1, uv, bv)):
                                ch = gv * 22 + j
                                S.op("pool", lambda e, u_=u_, ch=ch: e.tensor_copy(out=ubf[u_][:, 0:2], in_=halo[:, ch, :]), reads=[("halo", ch)], writes=[f"ub{u_}"])
                                S.op("act", lambda e, u_=u_, bank=bank: e.activation(out=ubf[u_][:, 2:514], in_=pb[bank][:, :], func=AF.Copy), reads=[f"pb{bank}"], writes=[f"ub{u_}"])
                                S.op("pool", lambda e, u_=u_, ch=ch: e.tensor_copy(out=halo[:, ch, :], in_=ubf[u_][:, 512:514]), reads=[f"ub{u_}"], writes=[("halo", ch)])
                                dst = cgv[:, gv * 512:(gv + 1) * 512]
                                S.op("act", lambda e, u_=u_, ch=ch, dst=dst: e.activation(out=dst, in_=ubf[u_][:, 2:514], func=AF.Identity, scale=cw[:, 88 + ch:89 + ch], bias=cb[:, ch:ch + 1]),
                                     reads=[f"ub{u_}", "pv"], writes=[("cgv", gv)])
                                S.op("dve", lambda e, u_=u_, ch=ch, dst=dst: e.scalar_tensor_tensor(out=dst, in0=ubf[u_][:, 1:513], scalar=cw[:, 44 + ch:45 + ch], in1=dst,
                                                                                                    op0=ALU.mult, op1=ALU.add), reads=[f"ub{u_}", "pv", ("cgv", gv)], writes=[("cgv", gv)])
                                S.op("dve", lambda e, u_=u_, ch=ch, dst=dst: e.scalar_tensor_tensor(out=dst, in0=ubf[u_][:, 0:512], scalar=cw[:, ch:ch + 1], in1=dst,
                                                                                                    op0=ALU.mult, op1=ALU.add), reads=[f"ub{u_}", "pv", ("cgv", gv)], writes=[("cgv", gv)])
                            S.op("act", lambda e: e.activation(out=gl[:], in_=cgv[:, 0:512], func=AF.Gelu_apprx_tanh), reads=[("cgv", 0)], writes=["gl"])
                            S.op("dve", lambda e, j=j: e.tensor_tensor(out=actT[:, j, :], in0=gl[:], in1=cgv[:, 512:1024], op=ALU.mult), reads=["gl", ("cgv", 1)], writes=["actT"])
                    S.op("sp", lambda e: e.dma_start(out=gpt[:], in_=grow[1].partition_broadcast(128)), writes=["gpt"], stream="ld_gpt")
                    for ps_ in range(2):
                        for jg in range(6):
                            ja = jg * 4
                            nj = min(4, 22 - ja)
                            Wd, rd = wload(lambda e, W, ja=ja, nj=nj: e.dma_start(out=W[:, 0:nj * 1024].rearrange("p (j n) -> p j n", j=nj),
                                                                                 in_=w_down_b[ja * 128:(ja + nj) * 128, :].rearrange("(j p) n -> p j n", p=128)), ["w_down_b"])
                            Wd3 = Wd[:, 0:nj * 1024].rearrange("p (j n) -> p j n", j=nj)

                            def fn(e, ja=ja, nj=nj, Wd3=Wd3, ps_=ps_):
                                out = []
                                for jj in range(nj):
                                    j = ja + jj
                                    for s2 in range(2):
                                        st_ = ps_ * 2 + s2
                                        for hf in range(2):
                                            out.append(e.matmul(pb[4 + s2 * 2 + hf][:, :], lhsT=actT[:, j, st_ * 128:(st_ + 1) * 128], rhs=Wd3[:, jj, hf * 512:(hf + 1) * 512],
                                                                start=(j == 0), stop=(j == 21)))
                                return out
                            S.op("pe", fn, reads=["actT", rd], writes=["pb4", "pb5", "pb6", "pb7"])
                        for s2 in range(2):
                            st_ = ps_ * 2 + s2
                            sub = tt * 4 + st_
                            r0 = sub * 128
                            b0 = 4 + s2 * 2
                            xi = xn[0] % 2
                            xn[0] += 1
                            S.op("sp", lambda e, xi=xi, r0=r0: e.dma_start(out=xt[xi][:], in_=y[r0:r0 + 128, :]), reads=[("y", sub)], writes=[f"xt{xi}"], stream=f"ldx{xi}")
                            col = 32 + s2 * 4
                            rs, rs_res = rms_scale([f"pb{b0}", f"pb{b0 + 1}"], [(pb[b0][:, :], 512), (pb[b0 + 1][:, :], 512)], col)
                            for hf in range(2):
                                S.op("dve", lambda e, b0=b0, hf=hf, rs=rs: e.scalar_tensor_tensor(out=cgv[:, hf * 512:(hf + 1) * 512], in0=pb[b0 + hf][:, :], scalar=rs,
                                                                                              in1=gpt[:, hf * 512:(hf + 1) * 512], op0=ALU.mult, op1=ALU.mult),
                                     reads=[f"pb{b0 + hf}", rs_res, "gpt"], writes=[("cgv", hf)])
                            S.op("pool", lambda e, xi=xi: e.tensor_tensor(out=xt[xi][:], in0=cgv[:], in1=xt[xi][:], op=ALU.add), reads=[("cgv", 0), ("cgv", 1), f"xt{xi}"], writes=[f"xt{xi}"])
                            S.op("pool", lambda e, xi=xi, r0=r0: e.dma_start(out=y[r0:r0 + 128, :], in_=xt[xi][:]), reads=[f"xt{xi}"], writes=[("y", sub)], stream=f"st_y_{xi}")
                for tt_ in range(NT):
                    p3_tile(tt_)
                S.emit(last=True)
        else:
            S.op("sp", lambda e: e.dma_start(out=xt[0][:], in_=x[0:128, :]), writes=["xt0"], stream="ldx0")
            S.op("sp", lambda e: e.dma_start(out=y[0:128, :], in_=xt[0][:]), reads=["xt0"], stream="st_y_0")
            S.emit(last=True)
    return nc


def _t5_bucket(dist):
    max_exact = 16
    n = np.asarray(dist, dtype=np.float32)
    large = max_exact + (np.log(np.maximum(n, 1.0) / max_exact) / np.log(2048 / max_exact) * (32 - max_exact)).astype(np.int32)
    large = np.minimum(large, 31)
    return np.where(n < max_exact, n.astype(np.int32), large).astype(np.int32)


def _consts(rel_bias, g_pre_mix, g_pre_ffn, conv_w, conv_b, g_post_mix, g_post_ffn):
    j = np.arange(128)[:, None]
    s = np.arange(128)[None, :]
    cst = np.zeros((128, 3, 128), np.float32)
    cst[:, 0, :] = np.eye(128, dtype=np.float32)
    cst[:, 1, :] = np.where(j >= s, -1.0, 0.0)
    cst[:, 2, :] = np.where(j >= s, NEGV, 0.0)
    i = np.arange(128)[:, None]
    c = np.arange(256)[None, :]
    jj = i + 128 - c
    valid = (jj >= 0) & (jj <= 128)
    jc = np.clip(jj, 0, 128)
    bm = np.zeros((128, 12, 256), np.float32)
    for h in range(12):
        d = DIL[h // 4]
        buckets = _t5_bucket(np.arange(129) * d)
        bias = rel_bias[buckets, h]
        bm[:, h, :] = np.where(valid, bias[jc], np.float32(NEGV))
    pv = np.zeros((128, 192), np.float32)
    pv[:, 0:8] = g_pre_mix.reshape(8, 128).T
    pv[:, 8:16] = g_pre_ffn.reshape(8, 128).T
    pv[:, 16:148] = conv_w.reshape(3, 44, 128).transpose(2, 0, 1).reshape(128, 132)
    pv[:, 148:192] = conv_b.reshape(44, 128).T
    grow = np.stack([g_post_mix, g_post_ffn]).astype(np.float32)
    return cst, bm, pv, grow


_NC_CACHE = {}


def kernel(x, g_pre_mix, w_in, rel_bias, w_branch_a, w_branch_b, w_out, g_post_mix,
           g_pre_ffn, w_up, conv_w, conv_b, w_down, g_post_ffn):
    x = np.asarray(x, np.float32)
    f = lambda a: np.ascontiguousarray(np.asarray(a, np.float32))
    cst, bm, pv, grow = _consts(f(rel_bias), f(g_pre_mix)[0], f(g_pre_ffn)[0], f(conv_w)[0], f(conv_b)[0],
                                f(g_post_mix)[0], f(g_post_ffn)[0])
    if "nc" not in _NC_CACHE:
        _NC_CACHE["nc"] = build()
    nc = _NC_CACHE["nc"]
    shared = {"w_in": f(w_in)[0], "w_a": f(w_branch_a)[0], "w_b": f(w_branch_b)[0], "w_out": f(w_out)[0],
              "w_up": f(w_up)[0], "w_down": f(w_down)[0], "cst": cst, "bm": bm, "pv": pv, "grow": grow}
    in_maps = [dict(shared, x=np.ascontiguousarray(x[b])) for b in range(8)]
    res = run_bass_kernel_spmd(nc, in_maps, core_ids=list(range(8)))
    return np.stack([np.asarray(r["y"], np.float32) for r in res.results], axis=0)
```
